# Optimizing a Trainium2 kernel written in Bass

```python
import math
import jax, jax.numpy as jnp
from jax import lax
import numpy as np

D_MODEL = 2048
BATCH = 8
SEQ = 4096
DEPTH = 1

CHUNK = 64
D_MIX = D_MODEL
ATTN_WIDTH = D_MIX // 2
SSD_WIDTH = D_MIX - ATTN_WIDTH
ATTN_HEAD_DIM = 128
ATTN_Q_HEADS = ATTN_WIDTH // ATTN_HEAD_DIM
ATTN_KV_HEADS = 2
ROPE_THETA = 500000.0
ROPE_PARTIAL_DIV = 4
IDX_HEADS = 16
IDX_HEAD_DIM = 64
IDX_TOPK_MAX = 256
Q_BLOCK = 128
SSD_HEAD_DIM = 64
SSD_HEADS = SSD_WIDTH // SSD_HEAD_DIM
SSD_GROUPS = 4
SSD_HEADS_PER_GROUP = SSD_HEADS // SSD_GROUPS
SSD_STATE = 128
SSD_CONV = 4
SSD_CHUNK = CHUNK
SSD_XBC_WIDTH = SSD_WIDTH + 2 * SSD_GROUPS * SSD_STATE
DEEPNORM_ALPHA = (2.0 * DEPTH) ** 0.25
DEEPNORM_BETA = (8.0 * DEPTH) ** -0.25
LN_EPS = 1e-5
RMS_EPS = 1e-5
IN_WIDTHS = (
    ATTN_Q_HEADS * ATTN_HEAD_DIM,
    ATTN_KV_HEADS * ATTN_HEAD_DIM,
    ATTN_KV_HEADS * ATTN_HEAD_DIM,
    ATTN_WIDTH,
    IDX_HEADS * IDX_HEAD_DIM,
    IDX_HEAD_DIM,
    IDX_HEADS,
    SSD_WIDTH,
    SSD_XBC_WIDTH,
    SSD_HEADS,
)
IN_TOTAL = sum(IN_WIDTHS)
VALUE_COL = 2

kernel_name = "hymba_dsa_ssd_deepnorm_block"


def _split_cols(u, widths):
    out, start = [], 0
    for w in widths:
        out.append(u[..., start:start + w])
        start += w
    return out


def _layer_norm(x, g, b):
    x32 = x.astype(jnp.float32)
    mu = jnp.mean(x32, axis=-1, keepdims=True)
    var = jnp.mean(jnp.square(x32 - mu), axis=-1, keepdims=True)
    y = (x32 - mu) * lax.rsqrt(var + LN_EPS) * g.astype(jnp.float32) + b.astype(jnp.float32)
    return y.astype(x.dtype)


def _rope_partial(t, pos):
    d = t.shape[-1]
    rot = d // ROPE_PARTIAL_DIV
    half = rot // 2
    inv = ROPE_THETA ** (-jnp.arange(half, dtype=jnp.float32) * 2.0 / rot)
    ang = pos[:, None] * inv[None, :]
    cos = jnp.cos(ang)[:, None, :]
    sin = jnp.sin(ang)[:, None, :]
    t32 = t.astype(jnp.float32)
    x1 = t32[..., :half]
    x2 = t32[..., half:rot]
    out = jnp.concatenate([x1 * cos - x2 * sin, x2 * cos + x1 * sin, t32[..., rot:]], axis=-1)
    return out.astype(t.dtype)


def _dsa_attention(q, k, v, q_idx, k_idx, w_idx):
    bsz, seq = q.shape[0], q.shape[1]
    topk = min(IDX_TOPK_MAX, seq // 4)
    rep = ATTN_Q_HEADS // ATTN_KV_HEADS
    scale = ATTN_HEAD_DIM ** -0.5
    key_chunk = jnp.arange(seq) // CHUNK
    k_idx32 = k_idx.astype(jnp.float32)
    gather = jax.vmap(lambda a, ix: a[ix])

    def block(i):
        start = i * Q_BLOCK
        sl = lambda a: lax.dynamic_slice_in_dim(a, start, Q_BLOCK, axis=1)
        qb, qib, wb = sl(q), sl(q_idx), sl(w_idx)
        q_chunk = (start + jnp.arange(Q_BLOCK)) // CHUNK
        logits = jax.nn.relu(jnp.einsum('bqhd,bsd->bqhs', qib.astype(jnp.float32), k_idx32))
        score = jnp.einsum('bqhs,bqh->bqs', logits, wb.astype(jnp.float32))
        admissible = key_chunk[None, :] <= q_chunk[:, None]
        score = jnp.where(admissible[None], score, -jnp.inf)
        _, sel = lax.top_k(score, topk)
        valid = key_chunk[sel] <= q_chunk[None, :, None]
        kg = gather(k, sel)
        vg = gather(v, sel)
        qg = qb.reshape(bsz, Q_BLOCK, ATTN_KV_HEADS, rep, ATTN_HEAD_DIM)
        s = jnp.einsum('bqgrd,bqkgd->bqgrk', qg, kg).astype(jnp.float32) * scale
        s = jnp.where(valid[:, :, None, None, :], s, -jnp.inf)
        p = jax.nn.softmax(s, axis=-1).astype(vg.dtype)
        o = jnp.einsum('bqgrk,bqkgd->bqgrd', p, vg)
        return o.reshape(bsz, Q_BLOCK, ATTN_Q_HEADS * ATTN_HEAD_DIM)

    out = lax.map(block, jnp.arange(seq // Q_BLOCK))
    return out.transpose(1, 0, 2, 3).reshape(bsz, seq, ATTN_Q_HEADS * ATTN_HEAD_DIM)


def _causal_depthwise_conv(u, w, b):
    c = u.shape[-1]
    y = lax.conv_general_dilated(u, w[:, None, :], window_strides=(1,), padding=[(SSD_CONV - 1, 0)],
                                 dimension_numbers=('NWC', 'WIO', 'NWC'), feature_group_count=c)
    return y + b


def _ssd_mixer(xbc, z, dt_raw, conv_w, conv_b, dt_bias, a_log, d_skip, norm_w):
    bsz, seq, _ = xbc.shape
    G, R, P, N, l = SSD_GROUPS, SSD_HEADS_PER_GROUP, SSD_HEAD_DIM, SSD_STATE, SSD_CHUNK
    nc = seq // l
    xbc = jax.nn.silu(_causal_depthwise_conv(xbc, conv_w, conv_b))
    xs, bm, cm = _split_cols(xbc, (SSD_WIDTH, G * N, G * N))
    xs = xs.astype(jnp.float32).reshape(bsz, nc, l, G, R, P)
    bm = bm.astype(jnp.float32).reshape(bsz, nc, l, G, N)
    cm = cm.astype(jnp.float32).reshape(bsz, nc, l, G, N)
    dt = jax.nn.softplus(dt_raw.astype(jnp.float32) + dt_bias.astype(jnp.float32))
    dt = dt.reshape(bsz, nc, l, G, R)
    a = -jnp.exp(a_log.astype(jnp.float32)).reshape(G, R)
    xdt = xs * dt[..., None]
    a_cum = jnp.cumsum(dt * a, axis=2)
    seg = a_cum[:, :, :, None] - a_cum[:, :, None, :]
    tril = jnp.tril(jnp.ones((l, l), dtype=bool))[:, :, None, None]
    lmat = jnp.exp(jnp.where(tril, seg, -jnp.inf))
    cb = jnp.einsum('bclgn,bcsgn->bclsg', cm, bm)
    y_diag = jnp.einsum('bclsg,bclsgr,bcsgrp->bclgrp', cb, lmat, xdt)
    decay = jnp.exp(a_cum[:, :, -1:] - a_cum)
    states = jnp.einsum('bcsgn,bcsgr,bcsgrp->bcgrpn', bm, decay, xdt)
    chunk_decay = jnp.exp(a_cum[:, :, -1])

    def step(h, inp):
        st, dec = inp
        return h * dec[..., None, None] + st, h

    h0 = jnp.zeros((bsz, G, R, P, N), jnp.float32)
    _, prev = lax.scan(step, h0, (jnp.moveaxis(states, 1, 0), jnp.moveaxis(chunk_decay, 1, 0)))
    prev = jnp.moveaxis(prev, 0, 1)
    y_off = jnp.einsum('bclgn,bcgrpn,bclgr->bclgrp', cm, prev, jnp.exp(a_cum))
    y = y_diag + y_off + d_skip.astype(jnp.float32).reshape(G, R)[:, :, None] * xs
    y = y.reshape(bsz, seq, SSD_WIDTH)
    g = (y * jax.nn.silu(z.astype(jnp.float32))).reshape(bsz, seq, G, SSD_WIDTH // G)
    g = g * lax.rsqrt(jnp.mean(jnp.square(g), axis=-1, keepdims=True) + RMS_EPS)
    return (g.reshape(bsz, seq, SSD_WIDTH) * norm_w.astype(jnp.float32)).astype(z.dtype)


def setup_inputs(seed: int = 0) -> dict:
    key = jax.random.key(seed)
    ks = jax.random.split(key, 12)
    x = jax.random.normal(ks[0], (BATCH, SEQ, D_MODEL), jnp.float32)
    blk_keys = jax.random.split(ks[1], len(IN_WIDTHS))
    blocks = []
    for j, w in enumerate(IN_WIDTHS):
        scale = D_MODEL ** -0.5 * (DEEPNORM_BETA if j == VALUE_COL else 1.0)
        blocks.append(jax.random.normal(blk_keys[j], (DEPTH, D_MODEL, w), jnp.float32) * scale)
    w_in = jnp.concatenate(blocks, axis=-1)
    w_out = jax.random.normal(ks[2], (DEPTH, D_MIX, D_MODEL), jnp.float32) * (D_MIX ** -0.5 * DEEPNORM_BETA)
    conv_w = jax.random.normal(ks[3], (DEPTH, SSD_CONV, SSD_XBC_WIDTH), jnp.float32) * SSD_CONV ** -0.5
    conv_b = 0.01 * jax.random.normal(ks[4], (DEPTH, SSD_XBC_WIDTH), jnp.float32)
    u = jax.random.uniform(ks[5], (DEPTH, SSD_HEADS), jnp.float32)
    dt0 = jnp.exp(u * (math.log(0.1) - math.log(0.001)) + math.log(0.001))
    dt_bias = dt0 + jnp.log(-jnp.expm1(-dt0))
    a_log = jnp.log(jax.random.uniform(ks[6], (DEPTH, SSD_HEADS), jnp.float32, 1.0, 16.0))
    d_skip = 1.0 + 0.1 * jax.random.normal(ks[7], (DEPTH, SSD_HEADS), jnp.float32)
    ssd_norm_w = 1.0 + 0.1 * jax.random.normal(ks[8], (DEPTH, SSD_WIDTH), jnp.float32)
    ln_g = 1.0 + 0.1 * jax.random.normal(ks[9], (DEPTH, D_MODEL), jnp.float32)
    ln_b = 0.01 * jax.random.normal(ks[10], (DEPTH, D_MODEL), jnp.float32)
    return {"x": x, "w_in": w_in, "w_out": w_out, "conv_w": conv_w, "conv_b": conv_b,
            "dt_bias": dt_bias, "a_log": a_log, "d_skip": d_skip, "ssd_norm_w": ssd_norm_w,
            "ln_g": ln_g, "ln_b": ln_b}


def reference(x, w_in, w_out, conv_w, conv_b, dt_bias, a_log, d_skip, ssd_norm_w, ln_g, ln_b):
    bsz, seq, _ = x.shape
    pos = jnp.arange(seq, dtype=jnp.float32)
    h = x
    for layer in range(DEPTH):
        u = jnp.einsum('bsd,de->bse', h, w_in[layer])
        q, k, v, z_attn, q_idx, k_idx, w_idx, z_ssd, xbc, dt_raw = _split_cols(u, IN_WIDTHS)
        q = _rope_partial(q.reshape(bsz, seq, ATTN_Q_HEADS, ATTN_HEAD_DIM), pos)
        k = _rope_partial(k.reshape(bsz, seq, ATTN_KV_HEADS, ATTN_HEAD_DIM), pos)
        v = v.reshape(bsz, seq, ATTN_KV_HEADS, ATTN_HEAD_DIM)
        q_idx = _rope_partial(q_idx.reshape(bsz, seq, IDX_HEADS, IDX_HEAD_DIM), pos)
        k_idx = _rope_partial(k_idx[:, :, None, :], pos)[:, :, 0, :]
        w_idx = w_idx * (IDX_HEADS ** -0.5 * IDX_HEAD_DIM ** -0.5)
        o_attn = _dsa_attention(q, k, v, q_idx, k_idx, w_idx) * jax.nn.silu(z_attn)
        o_ssd = _ssd_mixer(xbc, z_ssd, dt_raw, conv_w[layer], conv_b[layer], dt_bias[layer],
                           a_log[layer], d_skip[layer], ssd_norm_w[layer])
        mixed = jnp.concatenate([o_attn, o_ssd.astype(o_attn.dtype)], axis=-1)
        sub = jnp.einsum('bse,ed->bsd', mixed, w_out[layer])
        h = _layer_norm(DEEPNORM_ALPHA * h + sub, ln_g[layer], ln_b[layer])
    return h
```

```python
import numpy as np
import ml_dtypes
from contextlib import ExitStack
import concourse.bass as bass
import concourse.mybir as mybir
from concourse.bass_utils import run_bass_kernel_spmd

F32 = mybir.dt.float32
BF16 = mybir.dt.bfloat16
ALU = mybir.AluOpType
AF = mybir.ActivationFunctionType
AX = mybir.AxisListType

D = 2048
SEQ = 4096
NCORES = 8
IN_TOTAL = 6752
C_Q, C_K, C_V, C_ZA, C_QI, C_KI, C_WI, C_ZS, C_XBC, C_DT = (
    0, 1024, 1280, 1536, 2560, 3584, 3648, 3664, 4688, 6736)
ALPHA = 2.0 ** 0.25
LN_EPS = 1e-5
RMS_EPS = 1e-5
ROPE_THETA = 500000.0
NEG = -30000.0
PH_LIMIT = 3
NBIS = 22
ATT_SCALE = 128 ** -0.5


class Tile:
    def __init__(self, name, ap, psum=False):
        self.name = name
        self.ap = ap
        self.psum = psum
        self.wf = {}
        self.wp = {}
        self.r = {}

    def __getitem__(self, k):
        return self.ap[k]


def _tadd(d, tok):
    k = tok[:2]
    if d.get(k, 0) < tok[2]:
        d[k] = tok[2]


class Sched:
    ENG = ('pe', 'act', 'dve', 'pool', 'sp')

    def __init__(self, ndma=48):
        self.prog = {e: [] for e in self.ENG}
        self.cnt = {e: 0 for e in self.ENG}
        self.waited = {e: {} for e in self.ENG}
        self.ndma = ndma
        self.dma_pool = {'sp': list(range(0, ndma - 16)), 'pool': list(range(ndma - 16, ndma)),
                         'act': list(range(0, ndma - 16))}
        self.dma_k = {'sp': 0, 'pool': 0}
        self.dma_last = {}

    MAXOPS = None
    nops = 0

    def op(self, eng, fn, reads=(), writes=(), pwrites=(), dma=False):
        Sched.nops += 1
        if Sched.MAXOPS is not None and Sched.nops > Sched.MAXOPS:
            return None
        waits = {}
        for t in reads:
            for k, v in list(t.wf.items()) + list(t.wp.items()):
                if k[0] == 'e' and k[1] == 'pe' and eng == 'pe' and not dma:
                    continue
                _tadd(waits, k + (v,))
            if t.psum:
                for k, v in t.r.items():
                    if k[0] == 'e' and k[1] == eng:
                        continue
                    _tadd(waits, k + (v,))
        for t in writes:
            for k, v in list(t.wf.items()) + list(t.wp.items()) + list(t.r.items()):
                if k[0] == 'e' and k[1] == eng and eng == 'pe' and not dma:
                    continue
                _tadd(waits, k + (v,))
        for t in pwrites:
            for k, v in list(t.wf.items()) + list(t.r.items()):
                if k[0] == 'e' and k[1] == eng and eng == 'pe' and not dma:
                    continue
                _tadd(waits, k + (v,))
        if dma:
            pool_ = self.dma_pool[eng]
            kq = 'pool' if eng == 'pool' else 'sp'
            s = pool_[self.dma_k[kq] % len(pool_)]
            val = 16 * (self.dma_k[kq] // len(pool_) + 1)
            self.dma_k[kq] += 1
            if val > 16:
                _tadd(waits, ('d', s, val - 16))
            tok = ('d', s, val)
            self.dma_last[s] = val
            inc = ('d', s, 16)
        else:
            self.cnt[eng] += 1
            tok = ('e', eng, self.cnt[eng])
            inc = ('e', eng, 1)
        wl = []
        wd = self.waited[eng]
        for k, v in waits.items():
            if wd.get(k, 0) < v:
                wd[k] = v
                wl.append((k, v))
        self.prog[eng].append((wl, fn, inc))
        for t in reads:
            _tadd(t.r, tok)
        for t in writes:
            t.wf = {tok[:2]: tok[2]}
            t.wp = {}
            t.r = {}
        for t in pwrites:
            _tadd(t.wp, tok)
            t.r = {}
        return tok

    def barrier(self):
        for e in self.ENG:
            wl = []
            wd = self.waited[e]
            for e2 in self.ENG:
                if e2 == e:
                    continue
                k = ('e', e2)
                v = self.cnt[e2]
                if v > 0 and wd.get(k, 0) < v:
                    wd[k] = v
                    wl.append((k, v))
            for s, v in self.dma_last.items():
                k = ('d', s)
                if wd.get(k, 0) < v:
                    wd[k] = v
                    wl.append((k, v))
            if wl:
                self.prog[e].append((wl, None, None))

    def final_wait(self, eng='sp'):
        wl = []
        wd = self.waited[eng]
        for s, v in self.dma_last.items():
            k = ('d', s)
            if wd.get(k, 0) < v:
                wd[k] = v
                wl.append((k, v))
        for e2 in self.ENG:
            if e2 == eng:
                continue
            k = ('e', e2)
            v = self.cnt[e2]
            if v > 0 and wd.get(k, 0) < v:
                wd[k] = v
                wl.append((k, v))
        self.prog[eng].append((wl, None, None))

    def emit(self, nc, stack):
        esem = {e: stack.enter_context(nc.semaphore(f"es_{e}")) for e in self.ENG}
        dsem = [stack.enter_context(nc.semaphore(f"ds_{i}")) for i in range(self.ndma)]

        def semof(k):
            return esem[k[1]] if k[0] == 'e' else dsem[k[1]]

        prog = self.prog
        waited_vals = {e: set() for e in self.ENG}
        for name in self.ENG:
            for wl, fn, inc in prog[name]:
                for k, v in wl:
                    if k[0] == 'e':
                        waited_vals[k[1]].add(v)

        def run(engh, name):
            idx = 0
            last = 0
            for wl, fn, inc in prog[name]:
                for k, v in wl:
                    engh.wait_ge(semof(k), v)
                if fn is None:
                    continue
                ins = fn(engh)
                if inc[0] == 'd':
                    ins.then_inc(semof(inc), inc[2])
                else:
                    idx += 1
                    if idx in waited_vals[name]:
                        ins.then_inc(semof(inc), idx - last)
                        last = idx

        with nc.Block() as block:
            @block.tensor
            def _(e):
                run(e, 'pe')

            @block.scalar
            def _(e):
                run(e, 'act')

            @block.vector
            def _(e):
                run(e, 'dve')

            @block.gpsimd
            def _(e):
                run(e, 'pool')

            @block.sync
            def _(e):
                run(e, 'sp')


class Arena:
    def __init__(self, ap, nwords):
        self.ap = ap
        self.nwords = nwords
        self.off = 0

    def reset(self, off=0):
        self.off = off

    def alloc(self, name, shape, dtype):
        nfree = int(np.prod(shape[1:]))
        nbytes = nfree * (2 if dtype == BF16 else 4)
        nw = (nbytes + 3) // 4
        nw = (nw + 1) // 2 * 2
        assert self.off + nw <= self.nwords, f"SBUF arena overflow at {name}: {self.off}+{nw}>{self.nwords}"
        a = self.ap[0:shape[0], self.off:self.off + nw]
        self.off += nw
        if dtype == BF16:
            a = a.bitcast(BF16)
        a = a[:, 0:nfree]
        if len(shape) == 3:
            a = a.rearrange("p (a b) -> p a b", a=shape[1], b=shape[2])
        elif len(shape) == 4:
            a = a.rearrange("p (a b c) -> p a b c", a=shape[1], b=shape[2], c=shape[3])
        return Tile(name, a)


def build_nc(S, debug=False):
    NT = S // 128
    ST = min(S, 2048)
    NST = S // ST
    TOPK = min(256, S // 4)
    assert TOPK % 128 == 0 and S % 512 == 0

    nc = bass.Bass("TRN2", target_bir_lowering=False)
    stack = ExitStack()

    def din(name, shape, dt=F32):
        return nc.dram_tensor(name, list(shape), dt, kind="ExternalInput").ap()

    x_d = din("x", [S, D])
    win_d = din("w_in", [D, IN_TOTAL])
    wout_d = din("w_out", [D, D])
    convw_d = din("convw", [128, 16, 4])
    convb_d = din("convb", [128, 16])
    dtb_d = din("dt_bias", [16])
    alog_d = din("a_log", [16])
    dsk_d = din("d_skip", [16])
    nw_d = din("ssd_norm_w", [1024])
    lng_d = din("ln_g", [D])
    lnb_d = din("ln_b", [D])
    identb_d = din("c_identb", [128, 128], BF16)
    i4_d = din("c_i4", [128, 512], BF16)
    ra_d = din("c_ra", [128, 128], BF16)
    ri_d = din("c_ri", [128, 128], BF16)
    negd_d = din("c_negd", [128, 128], BF16)
    cosa_d = din("c_cosa", [128, S])
    sina_d = din("c_sina", [128, S])
    cosi_d = din("c_cosi", [128, S])
    sini_d = din("c_sini", [128, S])
    mats_d = din("c_mats", [128, 6, 128])
    ul_d = din("c_ul", [128, 64])
    sh_d = din("c_sh", [128, 64])
    pow2_d = din("c_pow2", [128, NBIS])

    out_d = nc.dram_tensor("out", [S, D], F32, kind="ExternalOutput").ap()

    skind = "ExternalOutput" if debug else "Internal"

    def dscr(name, shape, dt):
        return nc.dram_tensor(name, list(shape), dt, kind=skind).ap()

    qT_s = dscr("s_qT", [NT, 128, 8, 128], BF16)
    qiT_s = dscr("s_qiT", [NT, 128, 8, 128], BF16)
    kT_s = dscr("s_kT", [128, 2, S], BF16)
    kiT_s = dscr("s_kiT", [128, S], BF16)
    v_s = dscr("s_v", [S, 256], BF16)
    w_s = dscr("s_w", [S, 16], F32)
    zA_s = dscr("s_zA", [S, 1024], F32)
    zS_s = dscr("s_zS", [S, 1024], F32)
    xs_s = dscr("s_xs", [S, 1024], BF16)
    Bt_s = dscr("s_Bt", [S, 512], BF16)
    BT_s = dscr("s_BT", [4, 128, S], BF16)
    CT_s = dscr("s_CT", [4, 128, S], BF16)
    dt_s = dscr("s_dt", [S, 16], F32)
    mixT_s = dscr("s_mixT", [NT, 128, 16, 128], BF16)

    NW = 53000
    arena_t = stack.enter_context(nc.sbuf_tensor("arena", [128, NW], F32))
    AR = Arena(arena_t[:, :], NW)
    pbig = [stack.enter_context(nc.psum_tensor(f"pb{i}", [128, 1024], F32)) for i in range(4)]
    BK = [Tile(f"bank{i}", pbig[i // 2][:, (i % 2) * 512:(i % 2) * 512 + 512], psum=True) for i in range(8)]

    def pair_bf(i):
        return pbig[i][:, :].bitcast(BF16)

    SC = Sched()

    def DMA(q, out, in_, reads=(), writes=(), pwrites=()):
        return SC.op(q, lambda e: e.dma_start(out=out, in_=in_), reads=reads, writes=writes,
                     pwrites=pwrites, dma=True)

    def MM(out, lhsT, rhs, start, stop, reads, writes=(), pwrites=(), sgc=False):
        return SC.op('pe', lambda e: e.matmul(out, lhsT, rhs, start=start, stop=stop,
                                              skip_group_check=sgc),
                     reads=reads, writes=writes, pwrites=pwrites)

    def TR(out, in_, ident, reads, writes=(), pwrites=()):
        return SC.op('pe', lambda e: e.transpose(out, in_, ident), reads=reads, writes=writes,
                     pwrites=pwrites)

    def ACT(out, in_, func, reads, writes=(), pwrites=(), bias=None, scale=None, accum=None):
        def fn(e):
            kw = {}
            if bias is not None:
                kw['bias'] = bias
            if scale is not None:
                kw['scale'] = scale
            if accum is not None:
                kw['accum_out'] = accum
            return e.activation(out, in_, func, **kw)
        return SC.op('act', fn, reads=reads, writes=writes, pwrites=pwrites)

    def TT(eng, out, in0, in1, op, reads, writes=(), pwrites=()):
        return SC.op(eng, lambda e: e.tensor_tensor(out, in0, in1, op), reads=reads, writes=writes,
                     pwrites=pwrites)

    def TS(eng, out, in0, s1, s2, op0, op1, reads, writes=(), pwrites=(), accum=None):
        def fn(e):
            if accum is not None:
                return e.tensor_scalar(out, in0, s1, s2, op0, op1, accum_out=accum)
            if op1 is None:
                return e.tensor_scalar(out, in0, s1, None, op0)
            return e.tensor_scalar(out, in0, s1, s2, op0, op1)
        return SC.op(eng, fn, reads=reads, writes=writes, pwrites=pwrites)

    def STT(out, in0, scalar, in1, op0, op1, reads, writes=(), pwrites=()):
        return SC.op('dve', lambda e: e.scalar_tensor_tensor(out, in0, scalar, in1, op0, op1),
                     reads=reads, writes=writes, pwrites=pwrites)

    def CP(eng, out, in_, reads, writes=(), pwrites=()):
        return SC.op(eng, lambda e: e.tensor_copy(out, in_), reads=reads, writes=writes,
                     pwrites=pwrites)

    def MEMSET(eng, ap, val, writes=(), pwrites=()):
        return SC.op(eng, lambda e: e.memset(ap, val), writes=writes, pwrites=pwrites)

    def bc_last(ap, n):
        return ap.unsqueeze(2).broadcast_to([ap.shape[0], ap.shape[1], n])

    def bc_mid(ap, n):
        return ap.unsqueeze(1).broadcast_to([ap.shape[0], n, ap.shape[1]])

    identb = AR.alloc("identb", [128, 128], BF16)
    DMA('sp', identb.ap, identb_d, writes=[identb])
    PERSIST_OFF = AR.off

    xT = AR.alloc("xT", [128, 16, ST], BF16)
    xb = [AR.alloc(f"xb{i}", [128, D], BF16) for i in range(2)]
    cosa = AR.alloc("cosa", [128, ST], F32)
    sina = AR.alloc("sina", [128, ST], F32)
    cosi = AR.alloc("cosi", [128, ST], F32)
    sini = AR.alloc("sini", [128, ST], F32)
    ra = AR.alloc("ra", [128, 128], BF16)
    ri = AR.alloc("ri", [128, 128], BF16)
    convw = AR.alloc("convw", [128, 16, 4], F32)
    convb = AR.alloc("convb", [128, 16], F32)
    carry = AR.alloc("carry", [128, 16, 4], F32)
    wfm = [AR.alloc(f"wfm{i}", [128, 16, 128], BF16) for i in range(2)]
    wtm = [AR.alloc(f"wtm{i}", [128, 16, 256], BF16) for i in range(2)]
    qraw = [AR.alloc(f"qraw{i}", [128, 512], BF16) for i in range(2)]
    rt1 = [AR.alloc(f"rt1{i}", [128, 512], F32) for i in range(2)]
    rt2 = [AR.alloc(f"rt2{i}", [128, 512], F32) for i in range(2)]
    qo = [AR.alloc(f"qo{i}", [128, 512], BF16) for i in range(2)]
    cv = [AR.alloc(f"cv{i}", [128, 516], F32) for i in range(2)]
    acc = [AR.alloc(f"acc{i}", [128, 512], F32) for i in range(2)]
    xo = [AR.alloc(f"xo{i}", [128, 512], BF16) for i in range(2)]
    trs = [AR.alloc(f"trs{i}", [128, 4, 128], BF16) for i in range(2)]
    tms32 = [AR.alloc(f"tms32{i}", [128, 256], F32) for i in range(2)]
    tms16 = [AR.alloc(f"tms16{i}", [128, 256], BF16) for i in range(2)]

    DMA('sp', ra.ap, ra_d, writes=[ra])
    DMA('sp', ri.ap, ri_d, writes=[ri])
    DMA('sp', convw.ap, convw_d, writes=[convw])
    DMA('sp', convb.ap, convb_d, writes=[convb])

    cnt = {'fm': 0, 'tm': 0, 'mm': 0, 'ep': 0, 'tmm': 0, 'cv': 0}

    for st in range(NST):
        tok0 = st * ST
        DMA('sp', cosa.ap, cosa_d[:, tok0:tok0 + ST], writes=[cosa])
        DMA('sp', sina.ap, sina_d[:, tok0:tok0 + ST], writes=[sina])
        DMA('sp', cosi.ap, cosi_d[:, tok0:tok0 + ST], writes=[cosi])
        DMA('sp', sini.ap, sini_d[:, tok0:tok0 + ST], writes=[sini])
        for b in range(ST // 128):
            t0 = tok0 + b * 128
            xbt = xb[b % 2]
            DMA('pool', xbt.ap, x_d[t0:t0 + 128, :], writes=[xbt])
            pv = pair_bf(0)
            for kc in range(16):
                TR(pv[:, kc * 128:(kc + 1) * 128], xbt[:, kc * 128:(kc + 1) * 128], identb.ap,
                   reads=[xbt, identb],
                   writes=[BK[0], BK[1]] if kc == 0 else (),
                   pwrites=() if kc == 0 else [BK[0], BK[1]])
            eng = 'act' if b % 2 == 0 else 'dve'
            srcv = pv.rearrange("p (a b) -> p a b", a=16, b=128)
            if eng == 'act':
                ACT(xT[:, :, b * 128:(b + 1) * 128], srcv, AF.Copy, reads=[BK[0], BK[1]], pwrites=[xT])
            else:
                CP('dve', xT[:, :, b * 128:(b + 1) * 128], srcv, reads=[BK[0], BK[1]], pwrites=[xT])

        fm_list = []
        for h in range(8):
            fm_list.append(('ropeA', C_Q + h * 128, ('q', h)))
        for h in range(2):
            fm_list.append(('ropeA', C_K + h * 128, ('k', h)))
        for h in range(8):
            fm_list.append(('ropeI', C_QI + h * 128, ('qi', h)))
        fm_list.append(('ki', C_KI, ('ki', 0)))
        for c in range(16):
            fm_list.append(('conv', C_XBC + c * 128, ('xbc', c)))

        def load_fm(idx):
            kind, col0, _ = fm_list[idx]
            wb = wfm[cnt['fm'] % 2]
            cnt['fm'] += 1
            if kind == 'ki':
                src = win_d[:, col0:col0 + 64].rearrange("(kc p) e -> p kc e", p=128)
                DMA('pool', wb[:, :, 0:64], src, writes=[wb])
                DMA('pool', wb[:, :, 64:128], src, pwrites=[wb])
            else:
                src = win_d[:, col0:col0 + 128].rearrange("(kc p) e -> p kc e", p=128)
                DMA('pool', wb.ap, src, writes=[wb])
            return wb

        nxt = load_fm(0)
        for idx in range(len(fm_list)):
            kind, col0, (nm, hh) = fm_list[idx]
            wb = nxt
            if idx + 1 < len(fm_list):
                nxt = load_fm(idx + 1)
            for tt in range(ST // 512):
                gt0 = tok0 + tt * 512
                lsl = slice(tt * 512, (tt + 1) * 512)
                ps = BK[2 + cnt['mm'] % 2]
                cnt['mm'] += 1
                for kc in range(16):
                    MM(ps.ap, wb[:, kc, :], xT[:, kc, lsl], kc == 0, kc == 15,
                       reads=[wb, xT], writes=[ps] if kc == 0 else (),
                       pwrites=() if kc == 0 else [ps])
                e_i = cnt['ep'] % 2
                cnt['ep'] += 1
                if kind in ('ropeA', 'ropeI', 'ki'):
                    R = ra if kind == 'ropeA' else ri
                    ct = cosa if kind == 'ropeA' else cosi
                    sn = sina if kind == 'ropeA' else sini
                    qr = qraw[e_i]
                    ACT(qr.ap, ps.ap, AF.Copy, reads=[ps], writes=[qr])
                    ps2 = BK[4]
                    MM(ps2.ap, R.ap, qr.ap, True, True, reads=[R, qr], writes=[ps2])
                    TT('dve', rt1[e_i].ap, ps.ap, ct[:, lsl], ALU.mult, reads=[ps, ct], writes=[rt1[e_i]])
                    TT('dve', rt2[e_i].ap, ps2.ap, sn[:, lsl], ALU.mult, reads=[ps2, sn], writes=[rt2[e_i]])
                    TT('pool', qo[e_i].ap, rt1[e_i].ap, rt2[e_i].ap, ALU.add,
                       reads=[rt1[e_i], rt2[e_i]], writes=[qo[e_i]])
                    if nm == 'q' or nm == 'qi':
                        dst = (qT_s if nm == 'q' else qiT_s)[gt0 // 128:gt0 // 128 + 4, :, hh, :]
                        dst = dst.rearrange("b p t -> p b t")
                        DMA('sp', dst, qo[e_i].ap.rearrange("p (b t) -> p b t", b=4), reads=[qo[e_i]])
                    elif nm == 'k':
                        DMA('sp', kT_s[:, hh, gt0:gt0 + 512], qo[e_i].ap, reads=[qo[e_i]])
                    else:
                        DMA('sp', kiT_s[:, gt0:gt0 + 512], qo[e_i].ap, reads=[qo[e_i]])
                else:
                    c = hh
                    cvi = cnt['cv'] % 2
                    cnt['cv'] += 1
                    cvt = cv[cvi]
                    cvp = cv[1 - cvi]
                    if st == 0 and tt == 0:
                        MEMSET('dve', cvt[:, 0:3], 0.0, pwrites=[cvt])
                    elif tt == 0:
                        CP('dve', cvt[:, 0:3], carry[:, c, 0:3], reads=[carry], pwrites=[cvt])
                    else:
                        CP('dve', cvt[:, 0:3], cvp[:, 512:515], reads=[cvp], pwrites=[cvt])
                    ACT(cvt[:, 3:515], ps.ap, AF.Copy, reads=[ps], pwrites=[cvt])
                    if tt == ST // 512 - 1 and st < NST - 1:
                        CP('dve', carry[:, c, 0:3], cvt[:, 512:515], reads=[cvt], pwrites=[carry])
                    a = acc[e_i]
                    TS('dve', a.ap, cvt[:, 0:512], convw[:, c, 0:1], convb[:, c:c + 1], ALU.mult, ALU.add,
                       reads=[cvt, convw, convb], writes=[a])
                    for k in range(1, 4):
                        STT(a.ap, cvt[:, k:k + 512], convw[:, c, k:k + 1], a.ap, ALU.mult, ALU.add,
                            reads=[cvt, convw, a], writes=[a])
                    xot = xo[e_i]
                    ACT(xot.ap, a.ap, AF.Silu, reads=[a], writes=[xot])
                    if c >= 12:
                        DMA('sp', CT_s[c - 12, :, gt0:gt0 + 512], xot.ap, reads=[xot])
                    else:
                        if c >= 8:
                            DMA('sp', BT_s[c - 8, :, gt0:gt0 + 512], xot.ap, reads=[xot])
                        pt = BK[5]
                        ptv = pt.ap.bitcast(BF16)
                        for j in range(4):
                            TR(ptv[:, j * 128:(j + 1) * 128], xot[:, j * 128:(j + 1) * 128], identb.ap,
                               reads=[xot, identb], writes=[pt] if j == 0 else (),
                               pwrites=() if j == 0 else [pt])
                        stg = trs[e_i]
                        ACT(stg.ap, ptv[:, 0:512].rearrange("p (j e) -> p j e", j=4), AF.Copy,
                            reads=[pt], writes=[stg])
                        if c >= 8:
                            dst = Bt_s[gt0:gt0 + 512, (c - 8) * 128:(c - 7) * 128]
                        else:
                            dst = xs_s[gt0:gt0 + 512, c * 128:(c + 1) * 128]
                        DMA('sp', dst.rearrange("(j t) e -> t j e", t=128), stg.ap, reads=[stg])

        tm_list = [('v', C_V, 256, 0)]
        for j in range(4):
            tm_list.append(('zA', C_ZA + j * 256, 256, j * 256))
        tm_list.append(('w', C_WI, 16, 0))
        for j in range(4):
            tm_list.append(('zS', C_ZS + j * 256, 256, j * 256))
        tm_list.append(('dt', C_DT, 16, 0))

        def load_tm(idx):
            nm, col0, ncols, _ = tm_list[idx]
            wb = wtm[cnt['tm'] % 2]
            cnt['tm'] += 1
            src = win_d[:, col0:col0 + ncols].rearrange("(kc p) e -> p kc e", p=128)
            DMA('pool', wb[:, :, 0:ncols], src, writes=[wb])
            return wb

        nxt = load_tm(0)
        for idx in range(len(tm_list)):
            nm, col0, ncols, dcol = tm_list[idx]
            wb = nxt
            if idx + 1 < len(tm_list):
                nxt = load_tm(idx + 1)
            for blk in range(ST // 128):
                gt0 = tok0 + blk * 128
                ps = BK[6 + cnt['tmm'] % 2]
                e_i = cnt['tmm'] % 2
                cnt['tmm'] += 1
                for kc in range(16):
                    MM(ps[:, 0:ncols], xT[:, kc, blk * 128:(blk + 1) * 128], wb[:, kc, 0:ncols],
                       kc == 0, kc == 15, reads=[wb, xT], writes=[ps] if kc == 0 else (),
                       pwrites=() if kc == 0 else [ps])
                if nm == 'v':
                    sg = tms16[e_i]
                    ACT(sg[:, 0:ncols], ps[:, 0:ncols], AF.Copy, reads=[ps], writes=[sg])
                    DMA('sp', v_s[gt0:gt0 + 128, :], sg[:, 0:ncols], reads=[sg])
                else:
                    sg = tms32[e_i]
                    if nm in ('zA', 'zS'):
                        ACT(sg[:, 0:ncols], ps[:, 0:ncols], AF.Silu, reads=[ps], writes=[sg])
                        dst = (zA_s if nm == 'zA' else zS_s)[gt0:gt0 + 128, dcol:dcol + ncols]
                    elif nm == 'w':
                        ACT(sg[:, 0:ncols], ps[:, 0:ncols], AF.Copy, reads=[ps], writes=[sg], scale=1.0 / 32.0)
                        dst = w_s[gt0:gt0 + 128, :]
                    else:
                        ACT(sg[:, 0:ncols], ps[:, 0:ncols], AF.Copy, reads=[ps], writes=[sg])
                        dst = dt_s[gt0:gt0 + 128, :]
                    DMA('sp', dst, sg[:, 0:ncols], reads=[sg])

    SC.barrier()
    if PH_LIMIT < 2:
        SC.final_wait('sp'); SC.emit(nc, stack); stack.close(); return nc

    AR.reset(PERSIST_OFF)
    KT = AR.alloc("KT", [128, 2, S], BF16)
    KI = AR.alloc("KI", [128, S], BF16)
    V = AR.alloc("V", [128, NT, 2, 130], BF16)
    i4 = AR.alloc("i4", [128, 512], BF16)
    negd = AR.alloc("negd", [128, 128], BF16)
    mats = AR.alloc("mats", [128, 6, 128], F32)
    ul = AR.alloc("ul", [128, 64], F32)
    sh = AR.alloc("sh", [128, 64], F32)
    pow2 = AR.alloc("pow2", [128, NBIS], F32)
    a_bc = AR.alloc("a_bc", [128, 16], F32)
    dtb_bc = AR.alloc("dtb_bc", [128, 16], F32)
    dsk_bc = AR.alloc("dsk_bc", [128, 16], F32)
    nw_bc = AR.alloc("nw_bc", [128, 1024], F32)

    DMA('sp', KT.ap, kT_s, writes=[KT])
    DMA('sp', KI.ap, kiT_s, writes=[KI])
    MEMSET('dve', V[:, :, :, 128:130], 1.0, writes=[V])
    for g in range(2):
        for n0 in range(0, NT, 8):
            n1 = min(NT, n0 + 8)
            DMA('sp', V[:, n0:n1, g, 0:128],
                v_s[n0 * 128:n1 * 128, g * 128:(g + 1) * 128].rearrange("(n p) d -> p n d", p=128),
                pwrites=[V])
    DMA('sp', i4.ap, i4_d, writes=[i4])
    DMA('sp', negd.ap, negd_d, writes=[negd])
    DMA('sp', mats.ap, mats_d, writes=[mats])
    DMA('sp', ul.ap, ul_d, writes=[ul])
    DMA('sp', sh.ap, sh_d, writes=[sh])
    DMA('sp', pow2.ap, pow2_d, writes=[pow2])
    DMA('sp', a_bc.ap, alog_d.partition_broadcast(128), writes=[a_bc])
    DMA('sp', dtb_bc.ap, dtb_d.partition_broadcast(128), writes=[dtb_bc])
    DMA('sp', dsk_bc.ap, dsk_d.partition_broadcast(128), writes=[dsk_bc])
    DMA('sp', nw_bc.ap, nw_d.partition_broadcast(128), writes=[nw_bc])
    ACT(a_bc.ap, a_bc.ap, AF.Exp, reads=[a_bc], writes=[a_bc])
    TS('dve', a_bc.ap, a_bc.ap, -1.0, None, ALU.mult, None, reads=[a_bc], writes=[a_bc])

    M_U, M_NU, M_ONES, M_LS, M_OA, M_OB = range(6)

    def dbl(name, shape, dt):
        return [AR.alloc(f"{name}{i}", shape, dt) for i in range(2)]
    qb_t = dbl("qb", [128, 8, 128], BF16)
    qib_t = dbl("qib", [128, 8, 128], BF16)
    wq_t = dbl("wq", [128, 16], F32)
    zA_t = dbl("zA", [128, 1024], F32)
    zS_t = dbl("zS", [128, 1024], F32)
    xs_t = dbl("xs", [128, 16, 64], BF16)
    Bt_t = dbl("Bt", [128, 512], BF16)
    BTb_t = dbl("BTb", [128, 4, 128], BF16)
    CTp_t = dbl("CTp", [128, 4, 2, 128], BF16)
    dtr_t = dbl("dtr", [128, 16], F32)
    for t in CTp_t:
        MEMSET('dve', t.ap, 0.0, writes=[t])

    score = AR.alloc("score", [128, S], F32)
    negm = AR.alloc("negm", [128, S], BF16)
    rbuf = dbl("rbuf", [128, 512], F32)
    pTb = [AR.alloc(f"pT{i}", [128, 512], BF16) for i in range(3)]
    sm = AR.alloc("sm", [128, 64], F32)
    Wb = AR.alloc("Wb", [128, NBIS], F32)
    rec = AR.alloc("rec", [128, 8], F32)
    on_ = AR.alloc("on", [128, 1024], F32)
    mixed = AR.alloc("mixed", [128, 2048], BF16)
    dtx = AR.alloc("dtx", [128, 16], F32)
    dte = AR.alloc("dte", [128, 16], F32)
    dtv = AR.alloc("dtv", [128, 16], F32)
    dA = AR.alloc("dA", [128, 16], F32)
    rhs1 = AR.alloc("rhs1", [128, 16, 64], F32)
    rhs2 = AR.alloc("rhs2", [128, 16, 64], F32)
    LT = AR.alloc("LT", [128, 16, 64], BF16)
    Ex = AR.alloc("Ex", [128, 64], F32)
    MT = AR.alloc("MT", [128, 16, 128], BF16)
    MEMSET('dve', MT.ap, 0.0, writes=[MT])
    xdt = AR.alloc("xdt", [128, 16, 64], BF16)
    xdd = AR.alloc("xdd", [128, 16, 64], BF16)
    H = AR.alloc("H", [128, 16, 64], F32)
    Ht = AR.alloc("Ht", [128, 16, 64], F32)
    H2 = AR.alloc("H2", [128, 16, 64], F32)
    HbA = AR.alloc("HbA", [128, 1024], BF16)
    HbB = AR.alloc("HbB", [128, 1024], BF16)
    y1 = AR.alloc("y1", [128, 16, 64], F32)
    y2 = AR.alloc("y2", [128, 16, 64], F32)
    y3 = AR.alloc("y3", [128, 16, 64], F32)
    gt_ = AR.alloc("gt", [128, 1024], F32)
    junk = AR.alloc("junk", [128, 256], F32)
    ss = AR.alloc("ss", [128, 8], F32)
    mTs = dbl("mTs", [128, 16, 128], BF16)
    MEMSET('dve', H.ap, 0.0, writes=[H])
    MEMSET('dve', HbA.ap, 0.0, writes=[HbA])

    C_RMAX, C_RMIN, C_WID, C_LO, C_MID, C_CNT, C_T = range(7)

    def load_block(i):
        p = i % 2
        t0 = 128 * i
        DMA('sp', qb_t[p].ap, qT_s[i], writes=[qb_t[p]])
        DMA('sp', qib_t[p].ap, qiT_s[i], writes=[qib_t[p]])
        DMA('sp', wq_t[p].ap, w_s[t0:t0 + 128, :], writes=[wq_t[p]])
        DMA('sp', zA_t[p].ap, zA_s[t0:t0 + 128, :], writes=[zA_t[p]])
        DMA('sp', zS_t[p].ap, zS_s[t0:t0 + 128, :], writes=[zS_t[p]])
        DMA('sp', xs_t[p].ap.rearrange("p h d -> p (h d)"), xs_s[t0:t0 + 128, :], writes=[xs_t[p]])
        DMA('sp', Bt_t[p].ap, Bt_s[t0:t0 + 128, :], writes=[Bt_t[p]])
        DMA('sp', BTb_t[p].ap, BT_s[:, :, t0:t0 + 128].rearrange("g n t -> n g t"), writes=[BTb_t[p]])
        DMA('sp', CTp_t[p][:, :, 0, 0:64], CT_s[:, :, t0:t0 + 64].rearrange("g n t -> n g t"),
            pwrites=[CTp_t[p]])
        DMA('sp', CTp_t[p][:, :, 1, 64:128], CT_s[:, :, t0 + 64:t0 + 128].rearrange("g n t -> n g t"),
            pwrites=[CTp_t[p]])
        DMA('sp', dtr_t[p].ap, dt_s[t0:t0 + 128, :], writes=[dtr_t[p]])

    sidx = [0]

    def next_sbank():
        b = BK[sidx[0] % 2]
        sidx[0] += 1
        return b

    pidx = [0]

    load_block(0)
    for i in range(NT):
        p = i % 2
        if i + 1 < NT:
            load_block(i + 1)
        qb, qib, wq, zA, zS, xs, Bt, BTb, CTp, dtr = (qb_t[p], qib_t[p], wq_t[p], zA_t[p], zS_t[p],
                                                      xs_t[p], Bt_t[p], BTb_t[p], CTp_t[p], dtr_t[p])
        nk = 128 * (i + 1)
        sel = nk > TOPK
        if sel:
            assert (2 * i + 1) * 64 > TOPK
            nkt = (nk + 511) // 512
            for h in range(16):
                hp = (h % 2) * 64
                hc = h // 2
                for kt in range(nkt):
                    c0 = kt * 512
                    cw = min(512, nk - c0)
                    ps = next_sbank()
                    MM(ps[:, 0:cw], qib[hp:hp + 64, hc, :], KI[hp:hp + 64, c0:c0 + cw], True, True,
                       reads=[qib, KI], writes=[ps])
                    rb = rbuf[pidx[0] % 2]
                    pidx[0] += 1
                    ACT(rb[:, 0:cw], ps[:, 0:cw], AF.Relu, reads=[ps], writes=[rb])
                    if h == 0:
                        TS('dve', score[:, c0:c0 + cw], rb[:, 0:cw], wq[:, 0:1], None, ALU.mult, None,
                           reads=[rb, wq], writes=[score])
                    else:
                        STT(score[:, c0:c0 + cw], rb[:, 0:cw], wq[:, h:h + 1], score[:, c0:c0 + cw],
                            ALU.mult, ALU.add, reads=[rb, wq, score], writes=[score])
            SC.op('dve', lambda e, a=sm[:, C_RMAX:C_RMAX + 1], b=score[:, 0:nk]:
                  e.tensor_reduce(a, b, AX.X, ALU.max), reads=[score], writes=[sm])
            SC.op('dve', lambda e, a=sm[:, C_RMIN:C_RMIN + 1], b=score[:, 0:nk]:
                  e.tensor_reduce(a, b, AX.X, ALU.min), reads=[score, sm], writes=[sm])
            MEMSET('dve', score[0:64, nk - 64:nk], -1e30, writes=[score])
            TT('dve', sm[:, C_WID:C_WID + 1], sm[:, C_RMAX:C_RMAX + 1], sm[:, C_RMIN:C_RMIN + 1],
               ALU.subtract, reads=[sm], writes=[sm])
            TS('dve', Wb.ap, pow2.ap, sm[:, C_WID:C_WID + 1], None, ALU.mult, None,
               reads=[pow2, sm], writes=[Wb])
            CP('dve', sm[:, C_LO:C_LO + 1], sm[:, C_RMIN:C_RMIN + 1], reads=[sm], writes=[sm])
            for j in range(NBIS):
                TT('dve', sm[:, C_MID:C_MID + 1], sm[:, C_LO:C_LO + 1], Wb[:, j:j + 1], ALU.add,
                   reads=[sm, Wb], writes=[sm])
                TS('dve', negm[:, 0:nk], score[:, 0:nk], sm[:, C_MID:C_MID + 1], None, ALU.is_ge, ALU.add,
                   reads=[score, sm], writes=[negm, sm], accum=sm[:, C_CNT:C_CNT + 1])
                TS('dve', sm[:, C_T:C_T + 1], sm[:, C_CNT:C_CNT + 1], float(TOPK) - 0.5, Wb[:, j:j + 1],
                   ALU.is_ge, ALU.mult, reads=[sm, Wb], writes=[sm])
                TT('dve', sm[:, C_LO:C_LO + 1], sm[:, C_LO:C_LO + 1], sm[:, C_T:C_T + 1], ALU.add,
                   reads=[sm], writes=[sm])
            TS('dve', negm[:, 0:nk], score[:, 0:nk], sm[:, C_LO:C_LO + 1], NEG, ALU.is_lt, ALU.mult,
               reads=[score, sm], writes=[negm])

        for g in range(2):
            for j in range(i + 1):
                ps = next_sbank()
                masked = sel or (j == i)
                MM(ps.ap, KT[:, g, j * 128:(j + 1) * 128],
                   qb[:, 4 * g:4 * g + 4, :].rearrange("p h t -> p (h t)"), True, not masked,
                   reads=[KT, qb], writes=[ps])
                if masked:
                    if sel:
                        MM(ps.ap, negm[:, j * 128:(j + 1) * 128], i4.ap, False, True,
                           reads=[negm, i4], pwrites=[ps])
                    else:
                        MM(ps.ap, negd.ap, i4.ap, False, True, reads=[negd, i4], pwrites=[ps])
                pT = pTb[pidx[0] % 3]
                pidx[0] += 1
                ACT(pT.ap, ps.ap, AF.Exp, reads=[ps], writes=[pT], scale=ATT_SCALE)
                for r in range(4):
                    bk = BK[2 + r // 2]
                    col = (r % 2) * 129
                    first = (j == 0 and r % 2 == 0)
                    MM(bk[:, col:col + 129], pT[:, r * 128:(r + 1) * 128], V[:, j, g, 0:129],
                       first, j == i, reads=[pT, V],
                       writes=[bk] if first else (), pwrites=() if first else [bk], sgc=True)
            for b2 in range(2):
                bk = BK[2 + b2]
                hv = bk[:, 0:258].rearrange("p (h d) -> p h d", h=2)
                h0 = 4 * g + 2 * b2
                SC.op('dve', lambda e, o=rec[:, h0:h0 + 2], a=hv[:, :, 128]: e.reciprocal(o, a),
                      reads=[bk], writes=[rec])
                TT('dve', on_[:, h0 * 128:(h0 + 2) * 128].rearrange("p (h d) -> p h d", h=2),
                   hv[:, :, 0:128], bc_last(rec[:, h0:h0 + 2], 128), ALU.mult,
                   reads=[bk, rec], pwrites=[on_])
        TT('pool', mixed[:, 0:1024], on_.ap, zA.ap, ALU.mult, reads=[on_, zA], pwrites=[mixed])

        TT('dve', dtx.ap, dtr.ap, dtb_bc.ap, ALU.add, reads=[dtr, dtb_bc], writes=[dtx])
        ACT(dte.ap, dtx.ap, AF.Exp, reads=[dtx], writes=[dte])
        ACT(dtv.ap, dte.ap, AF.Ln, reads=[dte], writes=[dtv], bias=1.0)
        TT('dve', dA.ap, dtv.ap, a_bc.ap, ALU.mult, reads=[dtv, a_bc], writes=[dA])
        TT('pool', rhs1.ap, bc_last(dA.ap, 64), bc_mid(ul.ap, 16), ALU.mult, reads=[dA, ul], writes=[rhs1])
        TT('pool', rhs2.ap, bc_last(dA.ap, 64), bc_mid(sh.ap, 16), ALU.add, reads=[dA, sh], writes=[rhs2])
        smb = BK[7]
        for q_, mi in enumerate((M_U, M_LS, M_OA, M_OB)):
            MM(smb[:, q_ * 16:(q_ + 1) * 16], mats[:, mi, :], dA.ap, True, True, reads=[mats, dA],
               writes=[smb] if q_ == 0 else (), pwrites=() if q_ == 0 else [smb], sgc=True)
        ACT(Ex.ap, smb[:, 0:64], AF.Exp, reads=[smb], writes=[Ex])
        E_tok = Ex[:, 0:16]
        decay = Ex[:, 16:32]
        cdA = Ex[:, 32:48]
        cdB = Ex[:, 48:64]
        for b2 in range(2):
            sb_ = BK[4 + b2]
            MM(sb_.ap, mats[:, M_ONES, :], rhs1[:, 8 * b2:8 * b2 + 8, :].rearrange("p h l -> p (h l)"),
               True, False, reads=[mats, rhs1], writes=[sb_])
            MM(sb_.ap, mats[:, M_NU, :], rhs2[:, 8 * b2:8 * b2 + 8, :].rearrange("p h l -> p (h l)"),
               False, True, reads=[mats, rhs2], pwrites=[sb_])
            ACT(LT[:, 8 * b2:8 * b2 + 8, :].rearrange("p h l -> p (h l)"), sb_.ap, AF.Exp,
                reads=[sb_], pwrites=[LT])
        cbb = BK[6]
        for g in range(4):
            MM(cbb[:, g * 128:(g + 1) * 128], BTb[:, g, :], CTp[:, g, 0, :], g == 0, False,
               reads=[BTb, CTp], writes=[cbb] if g == 0 else (), pwrites=() if g == 0 else [cbb], sgc=True)
            MM(cbb[:, g * 128:(g + 1) * 128], BTb[:, g, :], CTp[:, g, 1, :], False, True,
               reads=[BTb, CTp], pwrites=[cbb], sgc=True)
        cbv = cbb.ap.rearrange("p (g l) -> p g l", g=4)
        for hf in range(2):
            ps_ = slice(hf * 64, hf * 64 + 64)
            in0 = cbv[ps_, :, hf * 64:hf * 64 + 64].unsqueeze(2).broadcast_to([64, 4, 4, 64])
            in1 = LT[ps_, :, :].rearrange("p (g r) l -> p g r l", g=4)
            o = MT[ps_, :, hf * 64:hf * 64 + 64].rearrange("p (g r) l -> p g r l", g=4)
            TT('dve', o, in0, in1, ALU.mult, reads=[cbb, LT], pwrites=[MT])
        TT('pool', xdt.ap, xs.ap, bc_last(dtv.ap, 64), ALU.mult, reads=[xs, dtv], writes=[xdt])
        TT('pool', xdd.ap, xdt.ap, bc_last(decay, 64), ALU.mult, reads=[xdt, Ex], writes=[xdd])
        for g in range(4):
            bk = BK[g // 2]
            MM(bk[:, (g % 2) * 256:(g % 2) * 256 + 256], Bt[0:64, g * 128:(g + 1) * 128],
               xdd[0:64, 4 * g:4 * g + 4, :].rearrange("p h d -> p (h d)"), True, True,
               reads=[Bt, xdd], writes=[bk] if g % 2 == 0 else (), pwrites=() if g % 2 == 0 else [bk],
               sgc=True)
        TT('pool', Ht.ap, H.ap, bc_last(cdA, 64), ALU.mult, reads=[H, Ex], writes=[Ht])
        for b2 in range(2):
            TT('dve', H2[:, 8 * b2:8 * b2 + 8, :].rearrange("p h d -> p (h d)"),
               Ht[:, 8 * b2:8 * b2 + 8, :].rearrange("p h d -> p (h d)"), BK[b2].ap, ALU.add,
               reads=[Ht, BK[b2]], pwrites=[H2])
        ACT(HbB.ap, H2.ap.rearrange("p h d -> p (h d)"), AF.Copy, reads=[H2], writes=[HbB])
        for g in range(4):
            bk = BK[2 + g // 2]
            MM(bk[:, (g % 2) * 256:(g % 2) * 256 + 256], Bt[64:128, g * 128:(g + 1) * 128],
               xdd[64:128, 4 * g:4 * g + 4, :].rearrange("p h d -> p (h d)"), True, True,
               reads=[Bt, xdd], writes=[bk] if g % 2 == 0 else (), pwrites=() if g % 2 == 0 else [bk],
               sgc=True)
        for h in range(16):
            bk = BK[4 + h // 8]
            MM(bk[:, (h % 8) * 64:(h % 8) * 64 + 64], MT[:, h, :], xdt[:, h, :], True, True,
               reads=[MT, xdt], writes=[bk] if h % 8 == 0 else (), pwrites=() if h % 8 == 0 else [bk],
               sgc=True)
        for g in range(4):
            bk = BK[6 + g // 2]
            cs = slice((g % 2) * 256, (g % 2) * 256 + 256)
            MM(bk[:, cs], CTp[:, g, 0, :], HbA[:, g * 256:(g + 1) * 256], True, False,
               reads=[CTp, HbA], writes=[bk] if g % 2 == 0 else (), pwrites=() if g % 2 == 0 else [bk],
               sgc=True)
            MM(bk[:, cs], CTp[:, g, 1, :], HbB[:, g * 256:(g + 1) * 256], False, True,
               reads=[CTp, HbB], pwrites=[bk], sgc=True)
        TT('pool', Ht.ap, H2.ap, bc_last(cdB, 64), ALU.mult, reads=[H2, Ex], writes=[Ht])
        for b2 in range(2):
            TT('dve', H[:, 8 * b2:8 * b2 + 8, :].rearrange("p h d -> p (h d)"),
               Ht[:, 8 * b2:8 * b2 + 8, :].rearrange("p h d -> p (h d)"), BK[2 + b2].ap, ALU.add,
               reads=[Ht, BK[2 + b2]], pwrites=[H])
        ACT(HbA.ap, H.ap.rearrange("p h d -> p (h d)"), AF.Copy, reads=[H], writes=[HbA])
        for b2 in range(2):
            hs = slice(8 * b2, 8 * b2 + 8)
            TT('dve', y1[:, hs, :], BK[6 + b2].ap.rearrange("p (h d) -> p h d", h=8),
               bc_last(E_tok[:, hs], 64), ALU.mult, reads=[BK[6 + b2], Ex], pwrites=[y1])
            TT('dve', y2[:, hs, :], y1[:, hs, :], BK[4 + b2].ap.rearrange("p (h d) -> p h d", h=8),
               ALU.add, reads=[y1, BK[4 + b2]], pwrites=[y2])
        TT('pool', y3.ap, xs.ap, bc_last(dsk_bc.ap, 64), ALU.mult, reads=[xs, dsk_bc], writes=[y3])
        TT('pool', y3.ap, y3.ap, y2.ap, ALU.add, reads=[y3, y2], writes=[y3])
        TT('pool', gt_.ap, y3.ap.rearrange("p h d -> p (h d)"), zS.ap, ALU.mult, reads=[y3, zS], writes=[gt_])
        for grp in range(4):
            ACT(junk.ap, gt_[:, grp * 256:(grp + 1) * 256], AF.Square, reads=[gt_], writes=[junk, ss],
                accum=ss[:, grp:grp + 1])
        TS('dve', ss[:, 4:8], ss[:, 0:4], 1.0 / 256.0, RMS_EPS, ALU.mult, ALU.add, reads=[ss], writes=[ss])
        ACT(ss[:, 4:8], ss[:, 4:8], AF.Sqrt, reads=[ss], writes=[ss])
        SC.op('dve', lambda e, o=ss[:, 4:8]: e.reciprocal(o, o), reads=[ss], writes=[ss])
        for grp in range(4):
            cs = slice(grp * 256, (grp + 1) * 256)
            STT(mixed[:, 1024 + grp * 256:1024 + (grp + 1) * 256], gt_[:, cs], ss[:, 4 + grp:5 + grp],
                nw_bc[:, cs], ALU.mult, ALU.mult, reads=[gt_, ss, nw_bc], pwrites=[mixed])
        for half in range(2):
            pv = pair_bf(half)
            bks = [BK[2 * half], BK[2 * half + 1]]
            for kk in range(8):
                kc = half * 8 + kk
                TR(pv[:, kk * 128:(kk + 1) * 128], mixed[:, kc * 128:(kc + 1) * 128], identb.ap,
                   reads=[mixed, identb], writes=bks if kk == 0 else (), pwrites=() if kk == 0 else bks)
            ACT(mTs[p][:, half * 8:half * 8 + 8, :], pv[:, 0:1024].rearrange("p (k t) -> p k t", k=8),
                AF.Copy, reads=bks, pwrites=[mTs[p]])
        DMA('sp', mixT_s[i], mTs[p].ap, reads=[mTs[p]])

    SC.barrier()
    if PH_LIMIT < 3:
        SC.final_wait('sp'); SC.emit(nc, stack); stack.close(); return nc

    AR.reset(PERSIST_OFF)
    WO = AR.alloc("WO", [128, 16, D], BF16)
    lng = AR.alloc("lng", [128, D], F32)
    lnb = AR.alloc("lnb", [128, D], F32)
    mT_t = dbl("mT", [128, 16, 128], BF16)
    xr_t = dbl("xr", [128, D], F32)
    hb_t = dbl("hb", [128, D], F32)
    st6 = AR.alloc("st6", [128, 4, 6], F32)
    mv = AR.alloc("mv", [128, 8], F32)
    for kc4 in range(4):
        DMA('pool', WO[:, kc4 * 4:(kc4 + 1) * 4, :],
            wout_d[kc4 * 512:(kc4 + 1) * 512, :].rearrange("(kc p) e -> p kc e", p=128),
            writes=[WO] if kc4 == 0 else (), pwrites=() if kc4 == 0 else [WO])
    DMA('sp', lng.ap, lng_d.partition_broadcast(128), writes=[lng])
    DMA('sp', lnb.ap, lnb_d.partition_broadcast(128), writes=[lnb])

    def load3(i):
        p = i % 2
        DMA('sp', mT_t[p].ap, mixT_s[i], writes=[mT_t[p]])
        DMA('sp', xr_t[p].ap, x_d[128 * i:128 * i + 128, :], writes=[xr_t[p]])

    out_toks = []
    load3(0)
    for i in range(NT):
        p = i % 2
        if i + 1 < NT:
            load3(i + 1)
        mT, xr, hb = mT_t[p], xr_t[p], hb_t[p]
        for dc in range(4):
            bk = BK[(i % 2) * 4 + dc]
            for kc in range(16):
                MM(bk.ap, mT[:, kc, :], WO[:, kc, dc * 512:(dc + 1) * 512], kc == 0, kc == 15,
                   reads=[mT, WO], writes=[bk] if kc == 0 else (), pwrites=() if kc == 0 else [bk])
            STT(hb[:, dc * 512:(dc + 1) * 512], xr[:, dc * 512:(dc + 1) * 512], ALPHA, bk.ap,
                ALU.mult, ALU.add, reads=[xr, bk], writes=[hb] if dc == 0 else (),
                pwrites=() if dc == 0 else [hb])
            SC.op('dve', lambda e, o=st6[:, dc, :], a=hb[:, dc * 512:(dc + 1) * 512]: e.bn_stats(o, a),
                  reads=[hb], writes=[st6] if dc == 0 else (), pwrites=() if dc == 0 else [st6])
        SC.op('dve', lambda e, o=mv[:, 0:2], a=st6.ap.rearrange("p a b -> p (a b)"): e.bn_aggr(o, a),
              reads=[st6], writes=[mv])
        TS('dve', mv[:, 2:3], mv[:, 1:2], LN_EPS, None, ALU.add, None, reads=[mv], writes=[mv])
        ACT(mv[:, 2:3], mv[:, 2:3], AF.Sqrt, reads=[mv], writes=[mv])
        SC.op('dve', lambda e, o=mv[:, 3:4], a=mv[:, 2:3]: e.reciprocal(o, a), reads=[mv], writes=[mv])
        TS('dve', mv[:, 4:5], mv[:, 0:1], mv[:, 3:4], -1.0, ALU.mult, ALU.mult, reads=[mv], writes=[mv])
        ACT(hb.ap, hb.ap, AF.Identity, reads=[hb, mv], writes=[hb], scale=mv[:, 3:4], bias=mv[:, 4:5])
        TT('pool', hb.ap, hb.ap, lng.ap, ALU.mult, reads=[hb, lng], writes=[hb])
        TT('dve', hb.ap, hb.ap, lnb.ap, ALU.add, reads=[hb, lnb], writes=[hb])
        DMA('sp', out_d[128 * i:128 * i + 128, :], hb.ap, reads=[hb])

    SC.final_wait('sp')
    SC.emit(nc, stack)
    stack.close()
    return nc


def _constants(S):
    bf = ml_dtypes.bfloat16
    c = {}
    c["c_identb"] = np.eye(128, dtype=np.float32).astype(bf)
    c["c_i4"] = np.tile(np.eye(128, dtype=np.float32), (1, 4)).astype(bf)
    pos = np.arange(S, dtype=np.float32)

    def tables(rot, blocks):
        half = rot // 2
        inv = (ROPE_THETA ** (-np.arange(half, dtype=np.float32) * 2.0 / rot)).astype(np.float32)
        ang = (pos[None, :] * inv[:, None]).astype(np.float32)
        cs = np.cos(ang).astype(np.float32)
        sn = np.sin(ang).astype(np.float32)
        cos_t = np.ones((128, S), np.float32)
        sin_t = np.zeros((128, S), np.float32)
        R = np.zeros((128, 128), np.float32)
        for b0 in blocks:
            cos_t[b0:b0 + half] = cs
            cos_t[b0 + half:b0 + rot] = cs
            sin_t[b0:b0 + half] = -sn
            sin_t[b0 + half:b0 + rot] = sn
            for pp in range(half):
                R[b0 + pp + half, b0 + pp] = 1.0
                R[b0 + pp, b0 + pp + half] = 1.0
        return cos_t, sin_t, R.astype(bf)

    c["c_cosa"], c["c_sina"], c["c_ra"] = tables(32, [0])
    c["c_cosi"], c["c_sini"], c["c_ri"] = tables(16, [0, 64])
    negd = np.zeros((128, 128), np.float32)
    negd[0:64, 64:128] = NEG
    c["c_negd"] = negd.astype(bf)
    j = np.arange(128)
    same = (j[:, None] // 64) == (j[None, :] // 64)
    U = (same & (j[:, None] <= j[None, :])).astype(np.float32)
    LS = (same & (j[:, None] > j[None, :])).astype(np.float32)
    ones = same.astype(np.float32)
    oa = np.repeat((j < 64).astype(np.float32)[:, None], 128, axis=1)
    ob = np.repeat((j >= 64).astype(np.float32)[:, None], 128, axis=1)
    c["c_mats"] = np.ascontiguousarray(np.stack([U, -U, ones, LS, oa, ob], axis=1)).astype(np.float32)
    jl = j % 64
    l = np.arange(64)
    c["c_ul"] = (jl[:, None] <= l[None, :]).astype(np.float32)
    c["c_sh"] = (-NEG * (jl[:, None] == (l[None, :] + 1))).astype(np.float32)
    c["c_pow2"] = np.repeat((2.0 ** -(np.arange(NBIS) + 1.0))[None, :], 128, axis=0).astype(np.float32)
    return c


_NC_CACHE = {}


def _get_nc(S, debug=False):
    key = (S, debug)
    if key not in _NC_CACHE:
        _NC_CACHE[key] = build_nc(S, debug)
    return _NC_CACHE[key]


def _in_map(xb, w_in, w_out, conv_w, conv_b, dt_bias, a_log, d_skip, ssd_norm_w, ln_g, ln_b, consts):
    m = {
        "x": np.ascontiguousarray(xb, dtype=np.float32),
        "w_in": w_in, "w_out": w_out,
        "convw": np.ascontiguousarray(conv_w.T.reshape(16, 128, 4).transpose(1, 0, 2)),
        "convb": np.ascontiguousarray(conv_b.reshape(16, 128).T),
        "dt_bias": dt_bias, "a_log": a_log, "d_skip": d_skip, "ssd_norm_w": ssd_norm_w,
        "ln_g": ln_g, "ln_b": ln_b,
    }
    m.update(consts)
    return m


def kernel(x, w_in, w_out, conv_w, conv_b, dt_bias, a_log, d_skip, ssd_norm_w, ln_g, ln_b):
    x = np.asarray(x, dtype=np.float32)
    B, S, _ = x.shape
    f = lambda a: np.ascontiguousarray(np.asarray(a, dtype=np.float32)[0])
    w_in, w_out, conv_w, conv_b = f(w_in), f(w_out), f(conv_w), f(conv_b)
    dt_bias, a_log, d_skip, ssd_norm_w, ln_g, ln_b = (f(dt_bias), f(a_log), f(d_skip), f(ssd_norm_w),
                                                      f(ln_g), f(ln_b))
    consts = _constants(S)
    nc = _get_nc(S)
    in_maps = [_in_map(x[b], w_in, w_out, conv_w, conv_b, dt_bias, a_log, d_skip, ssd_norm_w,
                       ln_g, ln_b, consts) for b in range(B)]
    res = run_bass_kernel_spmd(nc, in_maps, core_ids=list(range(B)))
    out = np.stack([np.asarray(r["out"], dtype=np.float32).reshape(S, D) for r in res.results], axis=0)
    return out
```
